# Optimizing a Trainium2 kernel written in Bass

```python
import math
import jax, jax.numpy as jnp
from jax import lax
import numpy as np

D_MODEL = 2048
BATCH = 2
SEQ = 16384
DEPTH = 2

CHUNK = 64
Q_BLOCK = 128
HEAD_DIM = 128
MIX_WIDTH = D_MODEL
N_MIX_HEADS = MIX_WIDTH // HEAD_DIM
A_HEADS = (3 * N_MIX_HEADS) // 8
B_HEADS = (N_MIX_HEADS - A_HEADS) // 2
C_HEADS = N_MIX_HEADS - A_HEADS - B_HEADS
A_QK_DIM = HEAD_DIM // 2
A_WIDTH = A_HEADS * HEAD_DIM
B_WIDTH = B_HEADS * HEAD_DIM
C_WIDTH = C_HEADS * HEAD_DIM
A_QK_COLS = A_HEADS * 2 * A_QK_DIM
IN_SPLITS = (A_QK_COLS, A_QK_COLS, A_WIDTH,
             B_WIDTH, B_WIDTH, B_WIDTH, B_WIDTH, B_HEADS,
             C_WIDTH, C_WIDTH, C_WIDTH)
IN_COLS = sum(IN_SPLITS)
C_LEFT_CHUNKS = 8
C_BAND = (C_LEFT_CHUNKS + 1) * CHUNK
REL_CLIP = 256
MEM_LEN = 256
MEM_HEADS = 4
MEM_WIDTH = MEM_HEADS * HEAD_DIM
D_FF = 5632
ROPE_THETA = 10000.0
EPS = 1e-6
NEG = -1e30

kernel_name = "hybrid_streaming_parallel_groups"


def rmsnorm(x, g):
    xf = x.astype(jnp.float32)
    y = xf * lax.rsqrt(jnp.mean(xf * xf, axis=-1, keepdims=True) + EPS)
    return (y * g.astype(jnp.float32)).astype(x.dtype)


def rope(x, positions):
    d = x.shape[-1]
    half = d // 2
    inv_freq = jnp.power(jnp.float32(ROPE_THETA), -jnp.arange(half, dtype=jnp.float32) / half)
    ang = positions.astype(jnp.float32)[..., None] * inv_freq
    cos = jnp.cos(ang)[:, :, None, :]
    sin = jnp.sin(ang)[:, :, None, :]
    xf = x.astype(jnp.float32)
    x1, x2 = xf[..., :half], xf[..., half:]
    out = jnp.concatenate([x1 * cos - x2 * sin, x2 * cos + x1 * sin], axis=-1)
    return out.astype(x.dtype)


def swiglu(u, w_gate, w_up, w_down):
    g = jnp.einsum('bsd,df->bsf', u, w_gate)
    v = jnp.einsum('bsd,df->bsf', u, w_up)
    return jnp.einsum('bsf,fd->bsd', jax.nn.silu(g) * v, w_down)


def diff_attention(q, k, v, lam):
    B, S, H, _, dq = q.shape
    nb = S // Q_BLOCK
    scale = dq ** -0.5
    qb = jnp.swapaxes(q.reshape(B, nb, Q_BLOCK, H, 2, dq), 0, 1)
    key_chunk = jnp.arange(S) // CHUNK

    def block(args):
        qi, bi = args
        s = jnp.einsum('bqhcd,bkhcd->bchqk', qi, k,
                       preferred_element_type=jnp.float32) * scale
        q_chunk = (bi * Q_BLOCK + jnp.arange(Q_BLOCK)) // CHUNK
        mask = key_chunk[None, :] <= q_chunk[:, None]
        p = jax.nn.softmax(jnp.where(mask, s, NEG), axis=-1)
        w = p[:, 0] - lam * p[:, 1]
        return jnp.einsum('bhqk,bkhd->bqhd', w.astype(v.dtype), v)

    o = lax.map(block, (qb, jnp.arange(nb)))
    return jnp.swapaxes(o, 0, 1).reshape(B, S, H, v.shape[-1])


def forgetting_attention(q, k, v, log_f):
    B, S, H, d = q.shape
    nb = S // Q_BLOCK
    scale = d ** -0.5
    F = jnp.cumsum(log_f, axis=1)
    Fk = jnp.transpose(F, (0, 2, 1))
    qb = jnp.swapaxes(q.reshape(B, nb, Q_BLOCK, H, d), 0, 1)
    Fqb = jnp.transpose(Fk.reshape(B, H, nb, Q_BLOCK), (2, 0, 1, 3))
    kpos = jnp.arange(S)

    def block(args):
        qi, Fq, bi = args
        s = jnp.einsum('bqhd,bkhd->bhqk', qi, k, preferred_element_type=jnp.float32) * scale
        s = s + Fq[..., None] - Fk[:, :, None, :]
        qpos = bi * Q_BLOCK + jnp.arange(Q_BLOCK)
        p = jax.nn.softmax(jnp.where(kpos[None, :] <= qpos[:, None], s, NEG), axis=-1)
        return jnp.einsum('bhqk,bkhd->bqhd', p.astype(v.dtype), v)

    o = lax.map(block, (qb, Fqb, jnp.arange(nb)))
    return jnp.swapaxes(o, 0, 1).reshape(B, S, H, d)


def chunk_band_attention(q, k, v, rel_bias):
    B, S, H, d = q.shape
    NC = S // CHUNK
    scale = d ** -0.5
    qc = q.reshape(B, NC, CHUNK, H, d)
    pad = ((0, 0), (C_LEFT_CHUNKS, 0), (0, 0), (0, 0), (0, 0))
    kp = jnp.pad(k.reshape(B, NC, CHUNK, H, d), pad)
    vp = jnp.pad(v.reshape(B, NC, CHUNK, H, d), pad)
    band_idx = jnp.arange(NC)[:, None] + jnp.arange(C_LEFT_CHUNKS + 1)[None, :]
    kb = kp[:, band_idx].reshape(B, NC, C_BAND, H, d)
    vb = vp[:, band_idx].reshape(B, NC, C_BAND, H, d)
    s = jnp.einsum('bnqhd,bnkhd->bnhqk', qc, kb, preferred_element_type=jnp.float32) * scale
    dist = C_LEFT_CHUNKS * CHUNK + jnp.arange(CHUNK)[:, None] - jnp.arange(C_BAND)[None, :]
    idx = jnp.clip(dist, -REL_CLIP, REL_CLIP) + REL_CLIP
    bias = rel_bias[:, idx].astype(jnp.float32)
    valid = (jnp.arange(NC)[:, None] - C_LEFT_CHUNKS
             + (jnp.arange(C_BAND) // CHUNK)[None, :]) >= 0
    s = jnp.where(valid[None, :, None, None, :], s + bias[None, None], NEG)
    p = jax.nn.softmax(s, axis=-1)
    o = jnp.einsum('bnhqk,bnkhd->bnqhd', p.astype(v.dtype), vb)
    return o.reshape(B, S, H, d)


def token_mixing(u, positions, w_in, w_out, lam_q1, lam_k1, lam_q2, lam_k2,
                 diff_subln_g, fox_forget_b, chunk_rel_bias, lam_init):
    B, S, _ = u.shape
    h = jnp.einsum('bsd,dc->bsc', u, w_in)
    cuts = np.cumsum(np.array(IN_SPLITS))[:-1].tolist()
    aq, ak, av, bq, bk, bv, bg, bf, cq, ck, cv = jnp.split(h, cuts, axis=-1)

    aq = rope(aq.reshape(B, S, A_HEADS * 2, A_QK_DIM), positions).reshape(B, S, A_HEADS, 2, A_QK_DIM)
    ak = rope(ak.reshape(B, S, A_HEADS * 2, A_QK_DIM), positions).reshape(B, S, A_HEADS, 2, A_QK_DIM)
    av = av.reshape(B, S, A_HEADS, HEAD_DIM)
    lam = (jnp.exp(jnp.sum(lam_q1.astype(jnp.float32) * lam_k1.astype(jnp.float32)))
           - jnp.exp(jnp.sum(lam_q2.astype(jnp.float32) * lam_k2.astype(jnp.float32)))
           + lam_init)
    oa = diff_attention(aq, ak, av, lam)
    oa = rmsnorm(oa, diff_subln_g) * (1.0 - lam_init)

    log_f = jax.nn.log_sigmoid(bf.astype(jnp.float32) + fox_forget_b.astype(jnp.float32))
    ob = forgetting_attention(bq.reshape(B, S, B_HEADS, HEAD_DIM),
                              bk.reshape(B, S, B_HEADS, HEAD_DIM),
                              bv.reshape(B, S, B_HEADS, HEAD_DIM), log_f)
    ob = ob.reshape(B, S, B_WIDTH) * jax.nn.sigmoid(bg)

    oc = chunk_band_attention(cq.reshape(B, S, C_HEADS, HEAD_DIM),
                              ck.reshape(B, S, C_HEADS, HEAD_DIM),
                              cv.reshape(B, S, C_HEADS, HEAD_DIM), chunk_rel_bias)

    o = jnp.concatenate([oa.reshape(B, S, A_WIDTH), ob, oc.reshape(B, S, C_WIDTH)], axis=-1)
    return jnp.einsum('bsc,cd->bsd', o, w_out)


def memory_attention(u, mem_n, w_q, w_kv, w_o):
    B, S, _ = u.shape
    M = mem_n.shape[1]
    q = jnp.einsum('bsd,dc->bsc', u, w_q).reshape(B, S, MEM_HEADS, HEAD_DIM)
    kv = jnp.einsum('bmd,dc->bmc', mem_n, w_kv)
    k = kv[..., :MEM_WIDTH].reshape(B, M, MEM_HEADS, HEAD_DIM)
    v = kv[..., MEM_WIDTH:].reshape(B, M, MEM_HEADS, HEAD_DIM)
    s = jnp.einsum('bshd,bmhd->bhsm', q, k, preferred_element_type=jnp.float32) * HEAD_DIM ** -0.5
    p = jax.nn.softmax(s, axis=-1)
    o = jnp.einsum('bhsm,bmhd->bshd', p.astype(v.dtype), v).reshape(B, S, MEM_WIDTH)
    return jnp.einsum('bsc,cd->bsd', o, w_o)


def setup_inputs(seed: int = 0) -> dict:
    key = jax.random.key(seed)
    ks = iter(jax.random.split(key, 40))
    L, D = DEPTH, D_MODEL

    def nrm(shape, scale):
        return jax.random.normal(next(ks), shape, jnp.float32) * scale

    def gain(shape):
        return 1.0 + nrm(shape, 0.05)

    x = nrm((BATCH, SEQ, D), 1.0)
    mem = nrm((BATCH, MEM_LEN, D), 1.0)
    offset = jax.random.randint(next(ks), (BATCH, 1), 0, 4096, dtype=jnp.int32)
    positions = (offset + jnp.arange(SEQ, dtype=jnp.int32)[None, :]).astype(jnp.int32)
    return {
        "x": x, "mem": mem, "positions": positions,
        "ffn1_pre_g": gain((L, D)), "ffn1_post_g": gain((L, D)),
        "ffn1_w_gate": nrm((L, D, D_FF), D ** -0.5),
        "ffn1_w_up": nrm((L, D, D_FF), D ** -0.5),
        "ffn1_w_down": nrm((L, D_FF, D), D_FF ** -0.5),
        "mix_pre_g": gain((L, D)), "mix_post_g": gain((L, D)),
        "w_in": nrm((L, D, IN_COLS), D ** -0.5),
        "w_out": nrm((L, MIX_WIDTH, D), MIX_WIDTH ** -0.5),
        "lam_q1": nrm((L, A_QK_DIM), 0.1), "lam_k1": nrm((L, A_QK_DIM), 0.1),
        "lam_q2": nrm((L, A_QK_DIM), 0.1), "lam_k2": nrm((L, A_QK_DIM), 0.1),
        "diff_subln_g": gain((L, HEAD_DIM)),
        "fox_forget_b": 2.0 + nrm((L, B_HEADS), 0.5),
        "chunk_rel_bias": nrm((L, C_HEADS, 2 * REL_CLIP + 1), 0.1),
        "mem_pre_g": gain((L, D)), "mem_post_g": gain((L, D)), "mem_kv_g": gain((L, D)),
        "w_mem_q": nrm((L, D, MEM_WIDTH), D ** -0.5),
        "w_mem_kv": nrm((L, D, 2 * MEM_WIDTH), D ** -0.5),
        "w_mem_o": nrm((L, MEM_WIDTH, D), MEM_WIDTH ** -0.5),
        "ffn2_pre_g": gain((L, D)), "ffn2_post_g": gain((L, D)),
        "ffn2_w_gate": nrm((L, D, D_FF), D ** -0.5),
        "ffn2_w_up": nrm((L, D, D_FF), D ** -0.5),
        "ffn2_w_down": nrm((L, D_FF, D), D_FF ** -0.5),
    }


def reference(x, mem, positions,
              ffn1_pre_g, ffn1_post_g, ffn1_w_gate, ffn1_w_up, ffn1_w_down,
              mix_pre_g, mix_post_g, w_in, w_out,
              lam_q1, lam_k1, lam_q2, lam_k2, diff_subln_g,
              fox_forget_b, chunk_rel_bias,
              mem_pre_g, mem_post_g, mem_kv_g, w_mem_q, w_mem_kv, w_mem_o,
              ffn2_pre_g, ffn2_post_g, ffn2_w_gate, ffn2_w_up, ffn2_w_down):
    for l in range(DEPTH):
        lam_init = 0.8 - 0.6 * math.exp(-0.3 * l)
        y = swiglu(rmsnorm(x, ffn1_pre_g[l]), ffn1_w_gate[l], ffn1_w_up[l], ffn1_w_down[l])
        x = x + 0.5 * rmsnorm(y, ffn1_post_g[l])
        y = token_mixing(rmsnorm(x, mix_pre_g[l]), positions, w_in[l], w_out[l],
                         lam_q1[l], lam_k1[l], lam_q2[l], lam_k2[l], diff_subln_g[l],
                         fox_forget_b[l], chunk_rel_bias[l], lam_init)
        x = x + rmsnorm(y, mix_post_g[l])
        y = memory_attention(rmsnorm(x, mem_pre_g[l]), rmsnorm(mem, mem_kv_g[l]),
                             w_mem_q[l], w_mem_kv[l], w_mem_o[l])
        x = x + rmsnorm(y, mem_post_g[l])
        y = swiglu(rmsnorm(x, ffn2_pre_g[l]), ffn2_w_gate[l], ffn2_w_up[l], ffn2_w_down[l])
        x = x + 0.5 * rmsnorm(y, ffn2_post_g[l])
    return x
```

```python
import contextlib
import numpy as np
import concourse.bass as bass
import concourse.mybir as mybir

F32 = mybir.dt.float32
BF16 = mybir.dt.bfloat16
I32 = mybir.dt.int32
AF = mybir.ActivationFunctionType
ALU = mybir.AluOpType
AX = mybir.AxisListType

SAME_ENGINE_SYNC = True


class Res:
    __slots__ = ("name", "writers", "readers")

    def __init__(self, name=""):
        self.name = name
        self.writers = {}
        self.readers = {}


class Op:
    __slots__ = ("eng", "fn", "deps", "sem", "amt", "need_inc", "stamp", "is_dma", "seq")

    def __init__(self, eng, fn, deps, sem=None, amt=1, is_dma=False):
        self.eng = eng
        self.fn = fn
        self.deps = deps
        self.sem = sem
        self.amt = amt
        self.need_inc = is_dma
        self.stamp = None
        self.is_dma = is_dma
        self.seq = 0


class Sem:
    def __init__(self, handle, name):
        self.h = handle
        self.name = name
        self.count = 0
        self.hist = []


class Builder:
    ENGS = ("pe", "act", "dve", "pool", "sp")

    def __init__(self, nc):
        self.nc = nc
        self.stack = contextlib.ExitStack()
        self.ops = {e: [] for e in self.ENGS}
        self.esem = {}
        for e in ("pe", "act", "dve", "pool"):
            self.esem[e] = self.sem("e_" + e)
        self.n_sems = 4
        self._uid = 0

    def sem(self, name):
        h = self.stack.enter_context(self.nc.semaphore(name))
        return Sem(h, name)

    def sbuf(self, name, shape, dt):
        return self.stack.enter_context(self.nc.sbuf_tensor("sb_" + name, list(shape), dt))

    def psum(self, name, shape, dt=F32):
        return self.stack.enter_context(self.nc.psum_tensor("pp_" + name, list(shape), dt))

    def op(self, eng, fn, r=(), w=(), wp=(), sem=None):
        deps = {}
        for x in r:
            for k, o_ in x.writers.items():
                deps[id(o_)] = o_
        for x in list(w) + list(wp):
            for k, o_ in x.writers.items():
                deps[id(o_)] = o_
            for k, o_ in x.readers.items():
                deps[id(o_)] = o_
        is_dma = sem is not None
        o = Op(eng, fn, list(deps.values()), sem=sem if is_dma else self.esem.get(eng), amt=16 if is_dma else 1,
               is_dma=is_dma)
        if not is_dma and eng == "sp":
            raise ValueError("sp engine only issues DMAs")
        key = ("dma:" + sem.name) if is_dma else eng
        for x in r:
            x.readers[key] = o
        for x in w:
            x.writers = {key: o}
            x.readers = {}
        for x in wp:
            x.writers[key] = o
            x.readers = {}
        self._uid += 1
        o.seq = self._uid
        if is_dma:
            sem.hist.append(o)
        self.ops[eng].append(o)
        return o

    def finish(self, final_waits_eng="sp", extra_final=()):
        for e in self.ENGS:
            for o in self.ops[e]:
                for d in o.deps:
                    if d.eng == o.eng and not d.is_dma:
                        if e == "pe" or not SAME_ENGINE_SYNC:
                            continue
                    d.need_inc = True
        for e in self.ENGS:
            for o in self.ops[e]:
                if o.is_dma:
                    o.sem.count += 16
                    o.stamp = o.sem.count
        for e in ("pe", "act", "dve", "pool"):
            ops = [o for o in self.ops[e] if not o.is_dma]
            cnt = 0
            for o in ops:
                if o.need_inc:
                    cnt += 1
                    o.stamp = cnt
            nxt = None
            for o in reversed(ops):
                if o.need_inc:
                    nxt = o.stamp
                else:
                    o.stamp = nxt
        all_dma_sems = {}
        for e in self.ENGS:
            for o in self.ops[e]:
                if o.is_dma:
                    all_dma_sems[o.sem.name] = o.sem
        nc = self.nc
        handles = {"pe": "tensor", "act": "scalar", "dve": "vector", "pool": "gpsimd", "sp": "sync"}
        stats = {}
        with nc.Block() as block:
            for e in self.ENGS:
                ops = self.ops[e]

                def body(eng, e=e, ops=ops):
                    waited = {}
                    nw = 0
                    for o in ops:
                        need = {}
                        for d in o.deps:
                            if d.eng == e and not d.is_dma:
                                if e == "pe" or not SAME_ENGINE_SYNC:
                                    continue
                            s = d.sem
                            v = d.stamp
                            if d.is_dma:
                                lo, hi = 0, len(s.hist)
                                while lo < hi:
                                    mid = (lo + hi) // 2
                                    if s.hist[mid].seq < o.seq:
                                        lo = mid + 1
                                    else:
                                        hi = mid
                                v = s.hist[lo - 1].stamp
                                assert v >= d.stamp
                            assert v is not None, (e, d.eng)
                            if need.get(s.name, (None, 0))[1] < v:
                                need[s.name] = (s, v)
                        for nm, (s, v) in need.items():
                            if waited.get(nm, 0) < v:
                                eng.wait_ge(s.h, v)
                                waited[nm] = v
                                nw += 1
                        inst = o.fn(eng)
                        if o.is_dma:
                            inst.then_inc(o.sem.h, 16)
                        elif o.need_inc:
                            inst.then_inc(o.sem.h, 1)
                    if e == final_waits_eng:
                        for nm, s in all_dma_sems.items():
                            if s.count > 0 and waited.get(nm, 0) < s.count:
                                eng.wait_ge(s.h, s.count)
                    stats[e] = (len(ops), nw)
                getattr(block, handles[e])(body)
        self.stats = stats
        self.stack.close()
        return stats

import numpy as np

D = 2048; DFF = 5632; T = 512; KC = 16; FC = 44
EPS = 1e-6
WSLOT = 8192
NSLOT = 4
PI = float(np.pi)

class Ctx:
    pass

def setup_common(b, c):
    nc = b.nc
    c.xt = b.sbuf("xt", [128, KC, T], F32); c.xt_r = [Res("xt%d" % i) for i in range(KC)]
    c.ub = b.sbuf("ub", [128, KC, T], BF16); c.ub_r = [Res("ub%d" % i) for i in range(KC)]
    c.ht = b.sbuf("ht", [128, FC, T], BF16); c.ht_r = [Res("ht%d" % i) for i in range(FC)]
    c.wslots = []
    for i in range(NSLOT):
        t = b.sbuf("ws%d" % i, [128, WSLOT], BF16)
        c.wslots.append((t, Res("ws%d" % i), b.sem("ws%d" % i)))
    c.wnext = 0
    c.ps = []
    for i in range(8):
        t = b.psum("ps%d" % i, [128, 512], F32)
        c.ps.append((t, Res("ps%d" % i)))
    c.ones = b.sbuf("ones", [128, 128], BF16); c.ones_r = Res("ones")
    c.epst = b.sbuf("epst", [128, 1], F32); c.eps_r = Res("eps")
    c.rstd = b.sbuf("rstd", [128, T], F32); c.rstd_r = Res("rstd")
    c.tmpf = [(b.sbuf("tmpf%d" % i, [128, T], F32), Res("tmpf%d" % i)) for i in range(4)]
    c.tmpf_n = 0
    c.sq = [(b.sbuf("sq%d" % i, [128, T], BF16), Res("sq%d" % i)) for i in range(3)]
    c.sq_n = 0
    c.stg = [(b.sbuf("stg%d" % i, [128, T], BF16), Res("stg%d" % i), b.sem("stg%d" % i)) for i in range(4)]
    c.stg_n = 0
    b.op("dve", lambda e: e.memset(c.ones[:], 1.0), w=[c.ones_r])
    b.op("dve", lambda e: e.memset(c.epst[:], EPS), w=[c.eps_r])

def mm(b, ps_ap, ps_r, lhsT, rhs, start, stop, r):
    if start:
        b.op("pe", lambda e: e.matmul(ps_ap, lhsT, rhs, start=True, stop=stop), r=r, w=[ps_r])
    else:
        b.op("pe", lambda e: e.matmul(ps_ap, lhsT, rhs, start=False, stop=stop), r=r, wp=[ps_r])

def rot(lst, ctr_holder, attr):
    i = getattr(ctr_holder, attr)
    setattr(ctr_holder, attr, (i + 1) % len(lst))
    return lst[i]

def wload(b, c, src_ap, nelem):
    t, r, s = c.wslots[c.wnext]
    c.wnext = (c.wnext + 1) % NSLOT
    b.op("pool", lambda e: e.dma_start(out=t[:, 0:nelem], in_=src_ap), w=[r], sem=s)
    return t, r

def rms_rstd(b, c, n_in, src_sq_fn, pst, pst_r):
    raise NotImplementedError

def finish_rstd(b, c, pst, pst_r, n_feat):
    tf, tf_r = rot(c.tmpf, c, "tmpf_n")
    b.op("act", lambda e: e.activation(out=tf[:], in_=pst[:], func=AF.Sqrt, bias=c.epst[:, 0:1], scale=1.0 / n_feat),
         r=[pst_r, c.eps_r], w=[tf_r])
    b.op("dve", lambda e: e.reciprocal(out=c.rstd[:], in_=tf[:]), r=[tf_r], w=[c.rstd_r])

def prenorm(b, c, gcol, pst_i=6):
    pst, pst_r = c.ps[pst_i]
    for i in range(KC):
        sq, sq_r = rot(c.sq, c, "sq_n")
        b.op("act", lambda e, i=i, sq=sq: e.activation(out=sq[:], in_=c.xt[:, i, :], func=AF.Square),
             r=[c.xt_r[i]], w=[sq_r])
        mm(b, pst[:], pst_r, c.ones[:], sq[:], i == 0, i == KC - 1, [sq_r, c.ones_r])
    finish_rstd(b, c, pst, pst_r, D)
    gt, g_r, g0 = gcol
    for i in range(KC):
        b.op("dve", lambda e, i=i: e.scalar_tensor_tensor(out=c.ub[:, i, :], in0=c.xt[:, i, :],
                                                        scalar=gt[:, g0 + i:g0 + i + 1], in1=c.rstd[:],
                                                        op0=ALU.mult, op1=ALU.mult),
             r=[c.xt_r[i], c.rstd_r, g_r], w=[c.ub_r[i]])

def postnorm_residual(b, c, gcol, coef_in_g=True):
    gt, g_r, g0 = gcol
    for i in range(KC):
        tf, tf_r = rot(c.tmpf, c, "tmpf_n")
        b.op("dve", lambda e, i=i, tf=tf: e.scalar_tensor_tensor(out=tf[:], in0=c.ub[:, i, :],
                                                               scalar=gt[:, g0 + i:g0 + i + 1], in1=c.rstd[:],
                                                               op0=ALU.mult, op1=ALU.mult),
             r=[c.ub_r[i], c.rstd_r, g_r], w=[tf_r])
        b.op("dve", lambda e, i=i, tf=tf: e.tensor_tensor(out=c.xt[:, i, :], in0=c.xt[:, i, :], in1=tf[:], op=ALU.add),
             r=[tf_r, c.xt_r[i]], w=[c.xt_r[i]])

def ffn(b, c, wg, wu, wd, gpre, gpost_half):
    prenorm(b, c, gpre)
    pgs = [c.ps[0], c.ps[1]]; pus = [c.ps[2], c.ps[3]]
    for u in range(FC // 4):
        sg_t, sg_r = wload(b, c, wg[u], 8192)
        su_t, su_r = wload(b, c, wu[u], 8192)
        for j in range(4):
            n = u * 4 + j
            pg, pg_r = pgs[n % 2]; pu, pu_r = pus[n % 2]
            for kc in range(KC):
                off = (j * KC + kc) * 128
                mm(b, pg[:], pg_r, sg_t[:, off:off + 128], c.ub[:, kc, :], kc == 0, kc == KC - 1, [sg_r, c.ub_r[kc]])
            for kc in range(KC):
                off = (j * KC + kc) * 128
                mm(b, pu[:], pu_r, su_t[:, off:off + 128], c.ub[:, kc, :], kc == 0, kc == KC - 1, [su_r, c.ub_r[kc]])
            tf, tf_r = rot(c.tmpf, c, "tmpf_n")
            b.op("act", lambda e, pg=pg, tf=tf: e.activation(out=tf[:], in_=pg[:], func=AF.Silu), r=[pg_r], w=[tf_r])
            b.op("dve", lambda e, n=n, pu=pu, tf=tf: e.tensor_tensor(out=c.ht[:, n, :], in0=tf[:], in1=pu[:], op=ALU.mult),
                 r=[tf_r, pu_r], w=[c.ht_r[n]])
    pds = [c.ps[4], c.ps[5]]
    pst, pst_r = c.ps[6]
    pend = None
    for i in range(KC):
        sd_t, sd_r = wload(b, c, wd[i], FC * 128)
        pd, pd_r = pds[i % 2]
        for kc in range(FC):
            mm(b, pd[:], pd_r, sd_t[:, kc * 128:(kc + 1) * 128], c.ht[:, kc, :], kc == 0, kc == FC - 1, [sd_r, c.ht_r[kc]])
        if pend is not None:
            pend()
        b.op("act", lambda e, i=i, pd=pd: e.activation(out=c.ub[:, i, :], in_=pd[:], func=AF.Copy), r=[pd_r], w=[c.ub_r[i]])
        sq, sq_r = rot(c.sq, c, "sq_n")
        b.op("act", lambda e, pd=pd, sq=sq: e.activation(out=sq[:], in_=pd[:], func=AF.Square), r=[pd_r], w=[sq_r])
        def pend(i=i, sq=sq, sq_r=sq_r):
            mm(b, pst[:], pst_r, c.ones[:], sq[:], i == 0, i == KC - 1, [sq_r, c.ones_r])
    pend()
    finish_rstd(b, c, pst, pst_r, D)
    postnorm_residual(b, c, gpost_half)

def rope_tables(b, c, pos_ap, tabs):
    HI = 6.28125; LO = 2 * float(np.pi) - 6.28125
    pi_t, pi_r = tabs["posi"]
    b.op("sp", lambda e: e.dma_start(out=pi_t[:], in_=pos_ap), w=[pi_r], sem=tabs["sem"])
    ang, ang_r = tabs["ang"]; m1, m1_r = tabs["m1"]; m2, m2_r = tabs["m2"]; ki, ki_r = tabs["ki"]
    b.op("dve", lambda e: e.tensor_copy(out=ang[:], in_=pi_t[:]), r=[pi_r], w=[ang_r])
    b.op("dve", lambda e: e.tensor_scalar(out=ang[:], in0=ang[:], scalar1=tabs["invf"], scalar2=None, op0=ALU.mult),
         r=[ang_r, tabs["c_r"]], w=[ang_r])
    b.op("dve", lambda e: e.tensor_scalar(out=m1[:], in0=ang[:], scalar1=1.0 / (2 * PI), scalar2=None, op0=ALU.mult), r=[ang_r], w=[m1_r])
    b.op("dve", lambda e: e.tensor_copy(out=ki[:], in_=m1[:]), r=[m1_r], w=[ki_r])
    b.op("dve", lambda e: e.tensor_copy(out=m1[:], in_=ki[:]), r=[ki_r], w=[m1_r])
    b.op("dve", lambda e: e.scalar_tensor_tensor(out=m2[:], in0=m1[:], scalar=-HI, in1=ang[:], op0=ALU.mult, op1=ALU.add),
         r=[m1_r, ang_r], w=[m2_r])
    b.op("dve", lambda e: e.scalar_tensor_tensor(out=m2[:], in0=m1[:], scalar=-LO, in1=m2[:], op0=ALU.mult, op1=ALU.add),
         r=[m1_r, m2_r], w=[m2_r])
    b.op("dve", lambda e: e.tensor_scalar(out=m1[:], in0=m2[:], scalar1=PI, scalar2=-2 * PI, op0=ALU.is_gt, op1=ALU.mult), r=[m2_r], w=[m1_r])
    b.op("dve", lambda e: e.tensor_tensor(out=m2[:], in0=m2[:], in1=m1[:], op=ALU.add), r=[m1_r, m2_r], w=[m2_r])
    S, S_r = tabs["S"]
    b.op("act", lambda e: e.activation(out=S[:], in_=m2[:], func=AF.Sin, scale=tabs["sgn"]), r=[m2_r, tabs["c_r"]], w=[S_r])
    b.op("dve", lambda e: e.tensor_scalar(out=ang[:], in0=m2[:], scalar1=0.5 * PI, scalar2=None, op0=ALU.add), r=[m2_r], w=[ang_r])
    b.op("dve", lambda e: e.tensor_scalar(out=m1[:], in0=ang[:], scalar1=PI, scalar2=-2 * PI, op0=ALU.is_gt, op1=ALU.mult), r=[ang_r], w=[m1_r])
    b.op("dve", lambda e: e.tensor_tensor(out=ang[:], in0=ang[:], in1=m1[:], op=ALU.add), r=[m1_r, ang_r], w=[ang_r])
    C, C_r = tabs["C"]
    b.op("act", lambda e: e.activation(out=C[:], in_=ang[:], func=AF.Sin), r=[ang_r], w=[C_r])
    Cq, Cq_r = tabs["Cq"]; Sq, Sq_r = tabs["Sq"]
    b.op("dve", lambda e: e.tensor_scalar(out=Cq[:], in0=C[:], scalar1=0.125, scalar2=None, op0=ALU.mult), r=[C_r], w=[Cq_r])
    b.op("dve", lambda e: e.tensor_scalar(out=Sq[:], in0=S[:], scalar1=0.125, scalar2=None, op0=ALU.mult), r=[S_r], w=[Sq_r])

def alloc_rope(b, c, consts_ap):
    tabs = {}
    for nm in ("ang", "m1", "m2", "S", "C", "Cq", "Sq"):
        tabs[nm] = (b.sbuf("rp_" + nm, [128, T], F32), Res("rp_" + nm))
    tabs["posi"] = (b.sbuf("rp_posi", [128, T], I32), Res("rp_posi"))
    tabs["ki"] = (b.sbuf("rp_ki", [128, T], I32), Res("rp_ki"))
    tabs["sem"] = b.sem("rp")
    ct = b.sbuf("rp_consts", [128, 4], F32); tabs["c_r"] = Res("rp_c")
    b.op("sp", lambda e: e.dma_start(out=ct[:], in_=consts_ap), w=[tabs["c_r"]], sem=b.sem("rpc"))
    tabs["invf"] = ct[:, 0:1]; tabs["sgn"] = ct[:, 1:2]; tabs["nsgnpi"] = ct[:, 2:3]; tabs["npi"] = ct[:, 3:4]
    return tabs

def stage_out(b, c, dst_ap, fill_fn, fill_r, eng):
    st, st_r, st_s = c.stg[c.stg_n]; c.stg_n = (c.stg_n + 1) % len(c.stg)
    b.op(eng, lambda e: fill_fn(e, st), r=fill_r, w=[st_r])
    return st, st_r, st_s

def inproj(b, c, winf, wint, tabs, fb_tile, fb_r, d, t0, SC):
    pqs = [c.ps[0], c.ps[1], c.ps[2], c.ps[3]]
    pn = 0
    t1hold = None
    def dest(n):
        if n < 12: return ("QG", n // 2, "rope")
        if n < 24: return ("KV", (n - 12) // 2, "rope")
        m = n - 24
        if m < 5: return ("QG", 6 + m, "scale")
        if m < 10: return ("KV", 6 + m - 5, "copy")
        if m < 15: return ("QG", 11 + m - 10, "scale")
        if m < 20: return ("KV", 11 + m - 15, "copy")
        if m < 26: return ("KV", 16 + m - 20, "copy")
        if m < 31: return ("KV", 22 + m - 26, "copy")
        if m < 36: return ("KV", 27 + m - 31, "copy")
        return ("QG", 16 + m - 36, "sigmoid")
    for u in range(17):
        s_t, s_r = wload(b, c, winf[u], 8192)
        for j in range(4):
            n = u * 4 + j
            if n >= 65:
                continue
            pq, pq_r = pqs[pn % 4]; pn += 1
            for kc in range(KC):
                off = (j * KC + kc) * 128
                mm(b, pq[:], pq_r, s_t[:, off:off + 128], c.ub[:, kc, :], kc == 0, kc == KC - 1, [s_r, c.ub_r[kc]])
            dn, chunk, kind = dest(n)
            if kind == "rope":
                isq = n < 12
                Ct, Ct_r = tabs["Cq"] if isq else tabs["C"]
                St, St_r = tabs["Sq"] if isq else tabs["S"]
                if n % 2 == 0:
                    tf, tf_r = rot(c.tmpf, c, "tmpf_n")
                    b.op("dve", lambda e, pq=pq, tf=tf, Ct=Ct: e.tensor_tensor(out=tf[:], in0=pq[:], in1=Ct[:], op=ALU.mult),
                         r=[pq_r, Ct_r], w=[tf_r])
                    t1hold = (tf, tf_r)
                    continue
                tf2, tf2_r = rot(c.tmpf, c, "tmpf_n")
                b.op("dve", lambda e, pq=pq, tf2=tf2, St=St: e.tensor_tensor(out=tf2[:], in0=pq[:], in1=St[:], op=ALU.mult),
                     r=[pq_r, St_r], w=[tf2_r])
                tf, tf_r = t1hold
                st, st_r, st_s = c.stg[c.stg_n]; c.stg_n = (c.stg_n + 1) % len(c.stg)
                b.op("dve", lambda e, tf=tf, tf2=tf2, st=st: e.tensor_tensor(out=st[:], in0=tf[:], in1=tf2[:], op=ALU.add),
                     r=[tf_r, tf2_r], w=[st_r])
            else:
                st, st_r, st_s = c.stg[c.stg_n]; c.stg_n = (c.stg_n + 1) % len(c.stg)
                if kind == "sigmoid":
                    b.op("act", lambda e, pq=pq, st=st: e.activation(out=st[:], in_=pq[:], func=AF.Sigmoid), r=[pq_r], w=[st_r])
                else:
                    scale = SC if kind == "scale" else 1.0
                    b.op("act", lambda e, pq=pq, st=st, scale=scale: e.activation(out=st[:], in_=pq[:], func=AF.Copy, scale=scale),
                         r=[pq_r], w=[st_r])
            b.op("sp", lambda e, st=st, chunk=chunk, dn=dn: e.dma_start(out=d[dn][chunk, :, t0:t0 + T], in_=st[:]),
                 r=[st_r], wp=[d[dn + "_r"]], sem=st_s)
    s_t, s_r = wload(b, c, wint, 128)
    pts = [c.ps[4], c.ps[5]]
    for m in range(4):
        pt, pt_r = pts[m % 2]
        for kc in range(KC):
            mm(b, pt[:, 0:5], pt_r, c.ub[:, kc, m * 128:(m + 1) * 128], s_t[:, kc * 8:kc * 8 + 5], kc == 0, kc == KC - 1, [s_r, c.ub_r[kc]])
        ktl = (t0 // 128) + m
        lf, lf_r, lf_s = c.lf[c.lf_n]; c.lf_n = (c.lf_n + 1) % len(c.lf)
        b.op("dve", lambda e, pt=pt, lf=lf: e.tensor_tensor(out=lf[:, 0:5], in0=pt[:, 0:5], in1=fb_tile[:, 0:5], op=ALU.add),
             r=[pt_r, fb_r], w=[lf_r])
        b.op("act", lambda e, lf=lf: e.activation(out=lf[:, 0:5], in_=lf[:, 0:5], func=AF.Exp, scale=-1.0), r=[lf_r], w=[lf_r])
        b.op("act", lambda e, lf=lf: e.activation(out=lf[:, 0:5], in_=lf[:, 0:5], func=AF.Ln, bias=1.0, scale=1.0), r=[lf_r], w=[lf_r])
        b.op("dve", lambda e, lf=lf: e.tensor_scalar(out=lf[:, 0:5], in0=lf[:, 0:5], scalar1=-1.0, scalar2=None, op0=ALU.mult),
             r=[lf_r], w=[lf_r])
        b.op("sp", lambda e, lf=lf, ktl=ktl: e.dma_start(out=d["LF"][:, ktl, :], in_=lf[:, 0:5]),
             r=[lf_r], wp=[d["LF_r"]], sem=lf_s)

import numpy as np

def evac_y_stats(b, c, pd, pd_r, ybuf, y_r, i, pst, pst_r, n_chunks, pend):
    if pend is not None:
        pend()
    b.op("act", lambda e: e.activation(out=ybuf, in_=pd[:], func=AF.Copy), r=[pd_r], w=[y_r])
    sq, sq_r = rot(c.sq, c, "sq_n")
    b.op("act", lambda e: e.activation(out=sq[:], in_=pd[:], func=AF.Square), r=[pd_r], w=[sq_r])
    def pend2():
        mm(b, pst[:], pst_r, c.ones[:], sq[:], i == 0, i == n_chunks - 1, [sq_r, c.ones_r])
    return pend2

def residual_from(b, c, ytile, y_rs, y0, gcol):
    gt, g_r, g0 = gcol
    for i in range(KC):
        tf, tf_r = rot(c.tmpf, c, "tmpf_n")
        b.op("dve", lambda e, i=i, tf=tf: e.scalar_tensor_tensor(out=tf[:], in0=ytile[:, y0 + i, :], scalar=gt[:, g0 + i:g0 + i + 1],
                                                               in1=c.rstd[:], op0=ALU.mult, op1=ALU.mult),
             r=[y_rs[y0 + i], c.rstd_r, g_r], w=[tf_r])
        b.op("dve", lambda e, i=i, tf=tf: e.tensor_tensor(out=c.xt[:, i, :], in0=c.xt[:, i, :], in1=tf[:], op=ALU.add),
             r=[tf_r, c.xt_r[i]], w=[c.xt_r[i]])

def proj_to_y(b, c, w_units, n_kc, src_tile, src_rs, src0, gcol):
    pds = [c.ps[4], c.ps[5]]
    pst, pst_r = c.ps[6]
    pend = None
    i = 0
    for (wap, nelem, nj) in w_units:
        s_t, s_r = wload(b, c, wap, nelem)
        for j in range(nj):
            pd, pd_r = pds[i % 2]
            for kc in range(n_kc):
                off = (j * n_kc + kc) * 128
                mm(b, pd[:], pd_r, s_t[:, off:off + 128], src_tile[:, src0 + kc, :], kc == 0, kc == n_kc - 1, [s_r, src_rs[src0 + kc]])
            pend = evac_y_stats(b, c, pd, pd_r, c.ht[:, i, :], c.ht_r[i], i, pst, pst_r, KC, pend)
            i += 1
    pend()
    finish_rstd(b, c, pst, pst_r, D)
    residual_from(b, c, c.ht, c.ht_r, 0, gcol)

def mem_prep(b, c, memT_ap, wk_ap, wv_ap, gkv, mm_):
    M = 256
    msem = mm_["sem"]
    b.op("sp", lambda e: e.dma_start(out=c.xt[:, :, 0:M], in_=memT_ap.rearrange("c p t -> p c t")), w=c.xt_r, sem=msem)
    pst, pst_r = c.ps[6]
    for i in range(KC):
        sq, sq_r = rot(c.sq, c, "sq_n")
        b.op("act", lambda e, i=i, sq=sq: e.activation(out=sq[:, 0:M], in_=c.xt[:, i, 0:M], func=AF.Square), r=[c.xt_r[i]], w=[sq_r])
        mm(b, pst[:, 0:M], pst_r, c.ones[:], sq[:, 0:M], i == 0, i == KC - 1, [sq_r, c.ones_r])
    tf, tf_r = rot(c.tmpf, c, "tmpf_n")
    b.op("act", lambda e: e.activation(out=tf[:, 0:M], in_=pst[:, 0:M], func=AF.Sqrt, bias=c.epst[:, 0:1], scale=1.0 / D), r=[pst_r, c.eps_r], w=[tf_r])
    b.op("dve", lambda e: e.reciprocal(out=c.rstd[:, 0:M], in_=tf[:, 0:M]), r=[tf_r], w=[c.rstd_r])
    gt, g_r, g0 = gkv
    for i in range(KC):
        b.op("dve", lambda e, i=i: e.scalar_tensor_tensor(out=c.ub[:, i, 0:M], in0=c.xt[:, i, 0:M], scalar=gt[:, g0 + i:g0 + i + 1],
                                                        in1=c.rstd[:, 0:M], op0=ALU.mult, op1=ALU.mult),
             r=[c.xt_r[i], c.rstd_r, g_r], w=[c.ub_r[i]])
    kT, kT_r = mm_["kT"]; vM, vM_r = mm_["vM"]
    s_t, s_r = wload(b, c, wk_ap, 8192)
    for j in range(4):
        pd, pd_r = c.ps[j % 2]
        for kc in range(KC):
            off = (j * KC + kc) * 128
            mm(b, pd[:, 0:M], pd_r, s_t[:, off:off + 128], c.ub[:, kc, 0:M], kc == 0, kc == KC - 1, [s_r, c.ub_r[kc]])
        b.op("act", lambda e, j=j, pd=pd: e.activation(out=kT[:, j, :], in_=pd[:, 0:M], func=AF.Copy), r=[pd_r], wp=[kT_r])
    s_t, s_r = wload(b, c, wv_ap, 8192)
    for mt in range(2):
        pd, pd_r = c.ps[2 + mt]
        for kc in range(KC):
            mm(b, pd[:], pd_r, c.ub[:, kc, mt * 128:(mt + 1) * 128], s_t[:, kc * 512:(kc + 1) * 512], kc == 0, kc == KC - 1, [s_r, c.ub_r[kc]])
        b.op("act", lambda e, mt=mt, pd=pd: e.activation(out=vM[:, mt, :], in_=pd[:], func=AF.Copy), r=[pd_r], wp=[vM_r])

def t2_stages(b, c, d, it, gt, g_r, GI, mm_):
    t0 = it * T
    b.op("sp", lambda e: e.dma_start(out=c.ub[:], in_=d["OT"][:, :, t0:t0 + T].rearrange("c p t -> p c t")), w=c.ub_r, sem=d["ot_sem"])
    gT, gT_r = mm_["gT"]
    b.op("sp", lambda e: e.dma_start(out=gT[:], in_=d["QG"][16:21, :, t0:t0 + T].rearrange("c p t -> p c t")), w=[gT_r], sem=d["ot_sem"])
    for h in range(5):
        b.op("dve", lambda e, h=h: e.tensor_tensor(out=c.ub[:, 6 + h, :], in0=c.ub[:, 6 + h, :], in1=gT[:, h, :], op=ALU.mult),
             r=[c.ub_r[6 + h], gT_r], w=[c.ub_r[6 + h]])
    proj_to_y(b, c, [(d["wout"][u], 8192, 4) for u in range(4)], KC, c.ub, c.ub_r, 0, (gt, g_r, GI["mix_post"]))
    prenorm(b, c, (gt, g_r, GI["mem_pre"]))
    s_t, s_r = wload(b, c, d["wmq"][0], 8192)
    for j in range(4):
        pd, pd_r = c.ps[j % 2]
        for kc in range(KC):
            off = (j * KC + kc) * 128
            mm(b, pd[:], pd_r, s_t[:, off:off + 128], c.ub[:, kc, :], kc == 0, kc == KC - 1, [s_r, c.ub_r[kc]])
        b.op("act", lambda e, j=j, pd=pd: e.activation(out=c.ht[:, 16 + j, :], in_=pd[:], func=AF.Copy, scale=128 ** -0.5), r=[pd_r], w=[c.ht_r[16 + j]])
    kT, kT_r = mm_["kT"]; vM, vM_r = mm_["vM"]
    for hd in range(4):
        pts = []
        for mt in range(2):
            ps_, ps_r = c.ps[mt]
            mm(b, ps_[:], ps_r, kT[:, hd, mt * 128:(mt + 1) * 128], c.ht[:, 16 + hd, :], True, True, [kT_r, c.ht_r[16 + hd]])
            p_t, p_r = rot(c.sq, c, "sq_n")
            b.op("act", lambda e, ps_=ps_, p_t=p_t: e.activation(out=p_t[:], in_=ps_[:], func=AF.Exp), r=[ps_r], w=[p_r])
            pts.append((p_t, p_r))
        po, po_r = c.ps[2]; pr, pr_r = c.ps[3]
        for mt in range(2):
            p_t, p_r = pts[mt]
            mm(b, po[:], po_r, vM[:, mt, hd * 128:(hd + 1) * 128], p_t[:], mt == 0, mt == 1, [vM_r, p_r])
        for mt in range(2):
            p_t, p_r = pts[mt]
            mm(b, pr[:], pr_r, c.ones[:], p_t[:], mt == 0, mt == 1, [c.ones_r, p_r])
        tf, tf_r = rot(c.tmpf, c, "tmpf_n")
        b.op("dve", lambda e, tf=tf: e.reciprocal(out=tf[:], in_=pr[:]), r=[pr_r], w=[tf_r])
        b.op("dve", lambda e, hd=hd, tf=tf: e.tensor_tensor(out=c.ht[:, 20 + hd, :], in0=po[:], in1=tf[:], op=ALU.mult), r=[po_r, tf_r], w=[c.ht_r[20 + hd]])
    proj_to_y(b, c, [(d["wmo"][0], 8192, 16)], 4, c.ht, c.ht_r, 20, (gt, g_r, GI["mem_post"]))
    ffn(b, c, d["wg2"], d["wu2"], d["wd2"], (gt, g_r, GI["ffn2_pre"]), (gt, g_r, GI["ffn2_post"]))

import numpy as np

NEGM = -30000.0

def emit_att(b, d, S, lam_init):
    nc = b.nc
    NKT = S // 128
    NLT = NKT // 4
    NLOC = NLT * 128
    NG = NLT // 4
    def col(kt): return (kt % 4) * NLOC + (kt // 4) * 128
    def sidx(kt): return (kt % 4) * NLT + kt // 4
    csem = b.sem("att_c")
    idf = b.sbuf("idf", [128, 128], F32); idf_r = Res("idf")
    idb = b.sbuf("idb", [128, 128], BF16); idb_r = Res("idb")
    trib = b.sbuf("trib", [128, 128], BF16); trib_r = Res("trib")
    triu = b.sbuf("triu", [128, 128], F32); triu_r = Res("triu")
    onesf = b.sbuf("onesf", [128, 128], F32); onesf_r = Res("onesf")
    trif = b.sbuf("trif", [128, 128], F32); trif_r = Res("trif")
    b.op("sp", lambda e: e.dma_start(out=idf[:], in_=d["idf"]), w=[idf_r], sem=b.sem("attc1"))
    b.op("sp", lambda e: e.dma_start(out=trif[:], in_=d["tri"]), w=[trif_r], sem=b.sem("attc2"))
    b.op("sp", lambda e: e.dma_start(out=triu[:], in_=d["tri"]), w=[triu_r], sem=b.sem("attc3"))
    b.op("dve", lambda e: e.tensor_copy(out=idb[:], in_=idf[:]), r=[idf_r], w=[idb_r])
    b.op("dve", lambda e: e.tensor_copy(out=trib[:], in_=trif[:]), r=[trif_r], w=[trib_r])
    b.op("dve", lambda e: e.memset(onesf[:], 1.0), w=[onesf_r])
    lamv = b.sbuf("lamv", [128, 256], F32); lamv_r = Res("lamv")
    lamc = b.sbuf("lamc", [128, 4], F32); lamc_r = Res("lamc")
    b.op("sp", lambda e: e.dma_start(out=lamv[:], in_=d["lamv"]), w=[lamv_r], sem=b.sem("attc4"))
    b.op("dve", lambda e: e.tensor_tensor(out=lamv[:, 0:64], in0=lamv[:, 0:64], in1=lamv[:, 64:128], op=ALU.mult), r=[lamv_r], w=[lamv_r])
    b.op("dve", lambda e: e.tensor_tensor(out=lamv[:, 128:192], in0=lamv[:, 128:192], in1=lamv[:, 192:256], op=ALU.mult), r=[lamv_r], w=[lamv_r])
    b.op("dve", lambda e: e.reduce_sum(out=lamc[:, 0:1], in_=lamv[:, 0:64], axis=AX.X), r=[lamv_r], w=[lamc_r])
    b.op("dve", lambda e: e.reduce_sum(out=lamc[:, 1:2], in_=lamv[:, 128:192], axis=AX.X), r=[lamv_r, lamc_r], w=[lamc_r])
    b.op("act", lambda e: e.activation(out=lamc[:, 0:2], in_=lamc[:, 0:2], func=AF.Exp), r=[lamc_r], w=[lamc_r])
    b.op("dve", lambda e: e.tensor_tensor(out=lamc[:, 2:3], in0=lamc[:, 0:1], in1=lamc[:, 1:2], op=ALU.subtract), r=[lamc_r], w=[lamc_r])
    b.op("dve", lambda e: e.tensor_scalar(out=lamc[:, 2:3], in0=lamc[:, 2:3], scalar1=float(lam_init), scalar2=None, op0=ALU.add), r=[lamc_r], w=[lamc_r])
    gsub = b.sbuf("gsub", [128, 1], F32); gsub_r = Res("gsub")
    b.op("sp", lambda e: e.dma_start(out=gsub[:], in_=d["gsub"]), w=[gsub_r], sem=b.sem("attc5"))
    b.op("dve", lambda e: e.tensor_scalar(out=gsub[:], in0=gsub[:], scalar1=float(1.0 - lam_init), scalar2=None, op0=ALU.mult), r=[gsub_r], w=[gsub_r])
    epsc = b.sbuf("epsc", [128, 1], F32); epsc_r = Res("epsc")
    b.op("dve", lambda e: e.memset(epsc[:], 1e-6), w=[epsc_r])
    psS = [(b.psum("aS%d" % i, [128, 512], F32), Res("aS%d" % i)) for i in range(4)]
    psO = [(b.psum("aO%d" % i, [128, 512], F32), Res("aO%d" % i)) for i in range(3)]
    psT = (b.psum("aT", [128, 1024], BF16), Res("aT"))
    def oacc(i):
        t, r = psO[i // 3]
        return t, r, (i % 3) * 160
    lf = b.sbuf("lf_all", [128, NKT, 5], F32); lf_r = Res("lf_all")
    b.op("sp", lambda e: e.dma_start(out=lf[:], in_=d["LF"]), w=[lf_r], sem=b.sem("attc6"))
    ftab = b.sbuf("ftab", [128, NKT, 5], F32); ftab_r = Res("ftab")
    nftab = b.sbuf("nftab", [128, NKT, 5], F32); nftab_r = Res("nftab")
    tot = b.sbuf("ftot", [128, NKT, 5], F32); tot_r = Res("ftot")
    ncol = NKT * 5
    lf2 = lf[:].rearrange("p k h -> p (k h)"); ft2 = ftab[:].rearrange("p k h -> p (k h)"); tot2 = tot[:].rearrange("p k h -> p (k h)")
    for c0 in range(0, ncol, 512):
        c1 = min(ncol, c0 + 512)
        pw, pw_r = psS[0]; pt_, pt_r = psS[1]
        b.op("pe", lambda e, c0=c0, c1=c1: e.matmul(pw[:, 0:c1 - c0], triu[:], lf2[:, c0:c1], start=True, stop=True), r=[triu_r, lf_r], w=[pw_r])
        b.op("pe", lambda e, c0=c0, c1=c1: e.matmul(pt_[:, 0:c1 - c0], onesf[:], lf2[:, c0:c1], start=True, stop=True), r=[onesf_r, lf_r], w=[pt_r])
        b.op("dve", lambda e, c0=c0, c1=c1: e.tensor_copy(out=ft2[:, c0:c1], in_=pw[:, 0:c1 - c0]), r=[pw_r], wp=[ftab_r])
        b.op("dve", lambda e, c0=c0, c1=c1: e.tensor_copy(out=tot2[:, c0:c1], in_=pt_[:, 0:c1 - c0]), r=[pt_r], wp=[tot_r])
    run = b.sbuf("frun", [128, 8], F32); run_r = Res("frun")
    b.op("dve", lambda e: e.memset(run[:], 0.0), w=[run_r])
    for ktg in range(NKT):
        kt = sidx(ktg)
        b.op("dve", lambda e, kt=kt: e.tensor_tensor(out=ftab[:, kt, :], in0=ftab[:, kt, :], in1=run[:, 0:5], op=ALU.add), r=[ftab_r, run_r], wp=[ftab_r])
        b.op("dve", lambda e, kt=kt: e.tensor_tensor(out=run[:, 0:5], in0=run[:, 0:5], in1=tot[:, kt, :], op=ALU.add), r=[tot_r, run_r], w=[run_r])
    b.op("dve", lambda e: e.tensor_scalar(out=nftab[:], in0=ftab[:], scalar1=-1.0, scalar2=None, op0=ALU.mult), r=[ftab_r], w=[nftab_r])
    oh = b.sbuf("oh", [128, 4], F32); oh_r = Res("oh")
    b.op("sp", lambda e: e.dma_start(out=oh[:], in_=d["oh"]), w=[oh_r], sem=b.sem("attc7"))
    fown = b.sbuf("fown", [128, NLT, 5], F32); fown_r = Res("fown")
    b.op("dve", lambda e: e.tensor_scalar(out=fown[:], in0=ftab[:, 0:NLT, :], scalar1=oh[:, 0:1], scalar2=None, op0=ALU.mult), r=[ftab_r, oh_r], w=[fown_r])
    for r_ in range(1, 4):
        b.op("dve", lambda e, r_=r_: e.scalar_tensor_tensor(out=fown[:], in0=ftab[:, r_ * NLT:(r_ + 1) * NLT, :], scalar=oh[:, r_:r_ + 1], in1=fown[:],
                                                            op0=ALU.mult, op1=ALU.add), r=[ftab_r, oh_r, fown_r], w=[fown_r])
    dmf = b.sbuf("dmf", [128, 1024], F32); dmf_r = Res("dmf")
    dmA = b.sbuf("dmA", [128, 4, 128], BF16); dmA_r = Res("dmA")
    dmB = b.sbuf("dmB", [128, 4, 128], BF16); dmB_r = Res("dmB")
    b.op("sp", lambda e: e.dma_start(out=dmf[:], in_=d["dmask"]), w=[dmf_r], sem=b.sem("attc8"))
    b.op("dve", lambda e: e.tensor_copy(out=dmA[:].rearrange("p a c -> p (a c)"), in_=dmf[:, 0:512]), r=[dmf_r], w=[dmA_r])
    b.op("dve", lambda e: e.tensor_copy(out=dmB[:].rearrange("p a c -> p (a c)"), in_=dmf[:, 512:1024]), r=[dmf_r], w=[dmB_r])
    cb = b.sbuf("cbias", [128, 5, 1024], F32); cb_r = Res("cbias")
    cm = b.sbuf("cmask", [128, 1024], F32); cm_r = Res("cmask")
    b.op("sp", lambda e: e.dma_start(out=cb[:], in_=d["cbias"].rearrange("h p c -> p h c")), w=[cb_r], sem=b.sem("attc9"))
    b.op("sp", lambda e: e.dma_start(out=cm[:], in_=d["cmask"]), w=[cm_r], sem=b.sem("attc10"))
    for h in range(5):
        b.op("dve", lambda e, h=h: e.tensor_tensor(out=cb[:, h, :], in0=cb[:, h, :], in1=cm[:], op=ALU.add), r=[cb_r, cm_r], wp=[cb_r])
    kT = [(b.sbuf("kT%d" % i, [128, S], BF16), Res("kT%d" % i), b.sem("kT%d" % i)) for i in range(2)]
    vT = (b.sbuf("vT", [128, S], BF16), Res("vT"), b.sem("vT"))
    Vh = b.sbuf("Vh", [128, NKT, 130], BF16); Vh_r = Res("Vh")
    b.op("dve", lambda e: e.memset(Vh[:, :, 128:130], 1.0), w=[Vh_r])
    qT = [(b.sbuf("qT%d" % i, [128, NLOC], BF16), Res("qT%d" % i), b.sem("qT%d" % i)) for i in range(2)]
    pT = [(b.sbuf("pT%d" % i, [128, 512], BF16), Res("pT%d" % i)) for i in range(6)]
    tmpS = [(b.sbuf("tS%d" % i, [128, 512], F32), Res("tS%d" % i)) for i in range(3)]
    fqb = (b.sbuf("fqb", [128, 512], F32), Res("fqb"))
    dg = [(b.sbuf("dg%d" % i, [128, 128], F32), Res("dg%d" % i)) for i in range(2)]
    osb = [(b.sbuf("osb%d" % i, [128, 128], BF16), Res("osb%d" % i)) for i in range(2)]
    of32 = [(b.sbuf("of%d" % i, [128, 128], F32), Res("of%d" % i)) for i in range(2)]
    sm = [(b.sbuf("sm%d" % i, [128, 8], F32), Res("sm%d" % i)) for i in range(2)]
    ostg = [(b.sbuf("ostg%d" % i, [128, 512], BF16), Res("ostg%d" % i), b.sem("ostg%d" % i)) for i in range(2)]
    OT_r = d["OT_r"]
    cnt = {"pT": 0, "S": 0, "tS": 0, "h": 0, "dg": 0, "o": 0, "ostg": 0, "ev": 0}
    def nxt(lst, key):
        i = cnt[key]; cnt[key] = (i + 1) % len(lst); return lst[i]

    def load_head(kv_k, kv_v, q_chunk, ranges, hslot):
        kt_t, kt_r, kt_s = kT[hslot % 2]
        q_t, q_r, q_s = qT[hslot % 2]
        v_t, v_r, v_s = vT
        for n_, (t_lo, t_hi) in enumerate(ranges):
            k_lo = t_lo * 128; k_hi = t_hi * 128
            fk = lambda e, k_lo=k_lo, k_hi=k_hi: e.dma_start(out=kt_t[:, k_lo:k_hi], in_=d["KV"][kv_k, :, k_lo:k_hi])
            fv = lambda e, k_lo=k_lo, k_hi=k_hi: e.dma_start(out=v_t[:, k_lo:k_hi], in_=d["KV"][kv_v, :, k_lo:k_hi])
            if n_ == 0:
                b.op("sp", fk, w=[kt_r], sem=kt_s); b.op("sp", fv, w=[v_r], sem=v_s)
            else:
                b.op("sp", fk, wp=[kt_r], sem=kt_s); b.op("sp", fv, wp=[v_r], sem=v_s)
        b.op("sp", lambda e: e.dma_start(out=q_t[:], in_=d["QG"][q_chunk, :, :]), w=[q_r], sem=q_s)
        tp, tp_r = psT
        groups = []
        for (t_lo, t_hi) in ranges:
            for g0_ in range(t_lo, t_hi, 4):
                groups.append((g0_, min(g0_ + 4, t_hi)))
        for (g0, g1) in groups:
            for kt in range(g0, g1):
                j = kt - g0
                fn = (lambda e, kt=kt, j=j: e.transpose(tp[:, j * 128:(j + 1) * 128], v_t[:, kt * 128:(kt + 1) * 128], idb[:]))
                if j == 0:
                    b.op("pe", fn, r=[v_r, idb_r], w=[tp_r])
                else:
                    b.op("pe", fn, r=[v_r, idb_r], wp=[tp_r])
            n = g1 - g0
            eng = "act" if (cnt["ev"] % 2 == 0) else "dve"; cnt["ev"] += 1
            if eng == "act":
                b.op("act", lambda e, g0=g0, n=n: e.activation(out=Vh[:, g0:g0 + n, 0:128], in_=tp[:, 0:n * 128].rearrange("p (a c) -> p a c", c=128), func=AF.Copy),
                     r=[tp_r], wp=[Vh_r])
            else:
                b.op("dve", lambda e, g0=g0, n=n: e.tensor_copy(out=Vh[:, g0:g0 + n, 0:128], in_=tp[:, 0:n * 128].rearrange("p (a c) -> p a c", c=128)),
                     r=[tp_r], wp=[Vh_r])
        return (kt_t, kt_r), (q_t, q_r)

    def finish_subtile(kind, acc_ids, jq, head, li):
        o_t, o_r = nxt(osb, "o")
        s_t, s_r = nxt(sm, "o")
        if kind == "A":
            a0, a0_r, c0 = oacc(acc_ids[0]); a1, a1_r, c1 = oacc(acc_ids[1])
            f_t, f_r = nxt(of32, "o")
            b.op("dve", lambda e: e.reciprocal(out=s_t[:, 0:1], in_=a0[:, c0 + 128:c0 + 129]), r=[a0_r], w=[s_r])
            b.op("dve", lambda e: e.reciprocal(out=s_t[:, 1:2], in_=a1[:, c1 + 128:c1 + 129]), r=[a1_r, s_r], w=[s_r])
            b.op("dve", lambda e: e.tensor_tensor(out=s_t[:, 1:2], in0=s_t[:, 1:2], in1=lamc[:, 2:3], op=ALU.mult), r=[s_r, lamc_r], w=[s_r])
            b.op("dve", lambda e: e.tensor_scalar(out=f_t[:], in0=a1[:, c1:c1 + 128], scalar1=s_t[:, 1:2], scalar2=None, op0=ALU.mult), r=[a1_r, s_r], w=[f_r])
            b.op("dve", lambda e: e.scalar_tensor_tensor(out=f_t[:], in0=a0[:, c0:c0 + 128], scalar=s_t[:, 0:1], in1=f_t[:], op0=ALU.mult, op1=ALU.subtract),
                 r=[a0_r, s_r, f_r], w=[f_r])
            j_t, j_r = nxt(dg, "dg")
            b.op("dve", lambda e: e.tensor_tensor(out=j_t[:], in0=f_t[:], in1=f_t[:], op=ALU.mult), r=[f_r], w=[j_r])
            b.op("dve", lambda e: e.reduce_sum(out=s_t[:, 2:3], in_=j_t[:], axis=AX.X), r=[j_r, s_r], w=[s_r])
            b.op("act", lambda e: e.activation(out=s_t[:, 3:4], in_=s_t[:, 2:3], func=AF.Ln, bias=epsc[:, 0:1], scale=1.0 / 128), r=[s_r, epsc_r], w=[s_r])
            b.op("act", lambda e: e.activation(out=s_t[:, 4:5], in_=s_t[:, 3:4], func=AF.Exp, scale=-0.5), r=[s_r], w=[s_r])
            b.op("dve", lambda e: e.tensor_scalar(out=o_t[:], in0=f_t[:], scalar1=s_t[:, 4:5], scalar2=None, op0=ALU.mult), r=[f_r, s_r], w=[o_r])
        else:
            a0, a0_r, c0 = oacc(acc_ids[0])
            b.op("dve", lambda e: e.reciprocal(out=s_t[:, 0:1], in_=a0[:, c0 + 128:c0 + 129]), r=[a0_r], w=[s_r])
            b.op("dve", lambda e: e.tensor_scalar(out=o_t[:], in0=a0[:, c0:c0 + 128], scalar1=s_t[:, 0:1], scalar2=None, op0=ALU.mult), r=[a0_r, s_r], w=[o_r])
        tp, tp_r = psT
        b.op("pe", lambda e: e.transpose(tp[:, 0:128], o_t[:], idb[:]), r=[o_r, idb_r], w=[tp_r])
        st, st_r, st_s = ostg[cnt["ostg"]]
        if kind == "A":
            fn = lambda e: e.activation(out=st[:, jq * 128:(jq + 1) * 128], in_=tp[:, 0:128], func=AF.Copy, scale=gsub[:, 0:1])
            rr = [tp_r, gsub_r]
        else:
            fn = lambda e: e.activation(out=st[:, jq * 128:(jq + 1) * 128], in_=tp[:, 0:128], func=AF.Copy)
            rr = [tp_r]
        if jq == 0:
            b.op("act", fn, r=rr, w=[st_r])
        else:
            b.op("act", fn, r=rr, wp=[st_r])
        if jq == 3:
            b.op("sp", lambda e: e.dma_start(out=d["OT"][head, :, li * 512:(li + 1) * 512], in_=st[:]), r=[st_r], wp=[OT_r], sem=st_s)
            cnt["ostg"] = (cnt["ostg"] + 1) % len(ostg)

    hslot = 0
    full = [(0, NKT)]
    def stair(kt, g):
        if kt < 16 * g:
            return 0, None
        return (kt - 16 * g) // 4, (kt - 16 * g) % 4
    def run_pipeline(items, stage1, stage2, la):
        pend = []
        for it_ in items:
            pend.append(stage1(it_))
            if len(pend) > la:
                stage2(pend.pop(0))
        while pend:
            stage2(pend.pop(0))
    for h in range(6):
        (k_t, k_r), (q_t, q_r) = load_head(h, 16 + h, h, full, hslot); hslot += 1
        for g in range(NG):
            started = set()
            def s1(kt, g=g, k_t=k_t, k_r=k_r, q_t=q_t, q_r=q_r):
                a0, r = stair(kt, g)
                ncol = (4 - a0) * 128
                kc0 = col(kt); si = sidx(kt)
                s0, s0_r = nxt(psS, "S"); s1_, s1_r = nxt(psS, "S")
                qs = q_t[:, g * 512 + a0 * 128:(g + 1) * 512]
                b.op("pe", lambda e: e.matmul(s0[:, 0:ncol], k_t[0:64, kc0:kc0 + 128], qs[0:64, :], start=True, stop=True), r=[k_r, q_r], w=[s0_r])
                b.op("pe", lambda e: e.matmul(s1_[:, 0:ncol], k_t[64:128, kc0:kc0 + 128], qs[64:128, :], start=True, stop=True), r=[k_r, q_r], w=[s1_r])
                p0, p0_r = nxt(pT, "pT"); p1, p1_r = nxt(pT, "pT")
                b.op("act", lambda e: e.activation(out=p0[:, 0:ncol], in_=s0[:, 0:ncol], func=AF.Exp), r=[s0_r], w=[p0_r])
                b.op("act", lambda e: e.activation(out=p1[:, 0:ncol], in_=s1_[:, 0:ncol], func=AF.Exp), r=[s1_r], w=[p1_r])
                if r is not None:
                    b.op("pool", lambda e: e.tensor_tensor(out=p0[:, 0:128], in0=p0[:, 0:128], in1=dmA[:, r, :], op=ALU.mult), r=[p0_r, dmA_r], wp=[p0_r])
                    b.op("pool", lambda e: e.tensor_tensor(out=p1[:, 0:128], in0=p1[:, 0:128], in1=dmA[:, r, :], op=ALU.mult), r=[p1_r, dmA_r], wp=[p1_r])
                return (kt, a0, si, ((p0, p0_r), (p1, p1_r)))
            def s2(info, g=g, h=h, started=started):
                kt, a0, si, ps_ = info
                for a_ in range(a0, 4):
                    last = (kt == 16 * g + 4 * a_ + 3)
                    for comp, (p_, p_r) in enumerate(ps_):
                        ac, ac_r, co = oacc(comp * 4 + a_)
                        bank = (comp * 4 + a_) // 3
                        first = bank not in started; started.add(bank)
                        b.op("pe", lambda e, ac=ac, co=co, p_=p_, a_=a_, first=first, last=last: e.matmul(
                            ac[:, co:co + 129], p_[:, (a_ - a0) * 128:(a_ - a0 + 1) * 128], Vh[:, si, 0:129], start=first, stop=last), r=[p_r, Vh_r], wp=[ac_r])
                    if last:
                        finish_subtile("A", (a_, 4 + a_), a_, h, g)
            run_pipeline(range(16 * g + 16), s1, s2, 1)
    for h in range(5):
        (k_t, k_r), (q_t, q_r) = load_head(6 + h, 22 + h, 6 + h, full, hslot); hslot += 1
        for g in range(NG):
            fq_t, fq_r = fqb
            sF, sF_r = nxt(psS, "S")
            for a_ in range(4):
                g_t, g_r = nxt(dg, "dg")
                b.op("dve", lambda e, g_t=g_t, a_=a_, g=g, h=h: e.tensor_scalar(out=g_t[:], in0=idf[:], scalar1=fown[:, 4 * g + a_, h:h + 1], scalar2=None, op0=ALU.mult),
                     r=[idf_r, fown_r], w=[g_r])
                fn = lambda e, g_t=g_t, a_=a_, sF=sF: e.matmul(sF[:, a_ * 128:(a_ + 1) * 128], onesf[:], g_t[:], start=True, stop=True)
                if a_ == 0:
                    b.op("pe", fn, r=[onesf_r, g_r], w=[sF_r])
                else:
                    b.op("pe", fn, r=[onesf_r, g_r], wp=[sF_r])
            b.op("dve", lambda e, sF=sF: e.tensor_copy(out=fq_t[:], in_=sF[:]), r=[sF_r], w=[fq_r])
            started = set()
            def s1(kt, g=g, h=h, k_t=k_t, k_r=k_r, q_t=q_t, q_r=q_r, fq_t=fq_t, fq_r=fq_r):
                a0, r = stair(kt, g)
                ncol = (4 - a0) * 128
                kc0 = col(kt); si = sidx(kt)
                s0, s0_r = nxt(psS, "S")
                qs = q_t[:, g * 512 + a0 * 128:(g + 1) * 512]
                b.op("pe", lambda e: e.matmul(s0[:, 0:ncol], k_t[:, kc0:kc0 + 128], qs, start=True, stop=True), r=[k_r, q_r], w=[s0_r])
                ts_t, ts_r = nxt(tmpS, "tS")
                b.op("dve", lambda e: e.tensor_tensor(out=ts_t[:, 0:ncol], in0=s0[:, 0:ncol], in1=fq_t[:, a0 * 128:512], op=ALU.add), r=[s0_r, fq_r], w=[ts_r])
                if r is not None:
                    b.op("dve", lambda e: e.tensor_tensor(out=ts_t[:, 0:128], in0=ts_t[:, 0:128], in1=dmf[:, 512 + r * 128:512 + (r + 1) * 128], op=ALU.add),
                         r=[ts_r, dmf_r], w=[ts_r])
                p0, p0_r = nxt(pT, "pT")
                b.op("act", lambda e: e.activation(out=p0[:, 0:ncol], in_=ts_t[:, 0:ncol], func=AF.Exp, bias=nftab[:, si, h:h + 1], scale=1.0),
                     r=[ts_r, nftab_r], w=[p0_r])
                return (kt, a0, si, p0, p0_r)
            def s2(info, g=g, h=h, started=started):
                kt, a0, si, p0, p0_r = info
                for a_ in range(a0, 4):
                    last = (kt == 16 * g + 4 * a_ + 3)
                    ac, ac_r, co = oacc(a_)
                    first = (a_ // 3) not in started; started.add(a_ // 3)
                    b.op("pe", lambda e, ac=ac, co=co, a_=a_, first=first, last=last: e.matmul(
                        ac[:, co:co + 129], p0[:, (a_ - a0) * 128:(a_ - a0 + 1) * 128], Vh[:, si, 0:129], start=first, stop=last), r=[p0_r, Vh_r], wp=[ac_r])
                    if last:
                        finish_subtile("B", (a_,), a_, 6 + h, g)
            run_pipeline(range(16 * g + 16), s1, s2, 2)
    for h in range(5):
        for g in range(NG):
            lo = max(0, 4 * g - 1); hi = 4 * g + 4
            ranges = [(r_ * NLT + lo, r_ * NLT + hi) for r_ in range(4)]
            (k_t, k_r), (q_t, q_r) = load_head(11 + h, 27 + h, 11 + h, ranges, hslot); hslot += 1
            started = set()
            items = []
            for a_ in range(4):
                i_ = 4 * g + a_
                tl = [(d_, r_) for d_ in range(2) for r_ in range(4) if i_ - 1 + d_ >= 0]
                for (d_, r_) in tl:
                    items.append((a_, i_, d_, r_))
            def s1(item, h=h, k_t=k_t, k_r=k_r, q_t=q_t, q_r=q_r):
                a_, i_, d_, r_ = item
                li = i_ - 1 + d_
                kc0 = r_ * NLOC + li * 128; si = r_ * NLT + li
                s0, s0_r = nxt(psS, "S")
                qs = q_t[:, i_ * 128:(i_ + 1) * 128]
                b.op("pe", lambda e: e.matmul(s0[:, 0:128], k_t[:, kc0:kc0 + 128], qs, start=True, stop=True), r=[k_r, q_r], w=[s0_r])
                ts_t, ts_r = nxt(tmpS, "tS")
                ci = (d_ * 4 + r_) * 128
                b.op("dve", lambda e: e.tensor_tensor(out=ts_t[:, 0:128], in0=s0[:, 0:128], in1=cb[:, h, ci:ci + 128], op=ALU.add),
                     r=[s0_r, cb_r], w=[ts_r])
                p0, p0_r = nxt(pT, "pT")
                b.op("act", lambda e: e.activation(out=p0[:, 0:128], in_=ts_t[:, 0:128], func=AF.Exp), r=[ts_r], w=[p0_r])
                return (a_, d_, r_, si, p0, p0_r)
            def s2(info, g=g, h=h, started=started):
                a_, d_, r_, si, p0, p0_r = info
                ac, ac_r, co = oacc(a_)
                first = (a_ // 3) not in started; started.add(a_ // 3)
                last = (d_ == 1 and r_ == 3)
                b.op("pe", lambda e: e.matmul(ac[:, co:co + 129], p0[:, 0:128], Vh[:, si, 0:129], start=first, stop=last), r=[p0_r, Vh_r], wp=[ac_r])
                if last:
                    finish_subtile("C", (a_,), a_, 11 + h, g)
            run_pipeline(items, s1, s2, 2)

import time as _time
from concourse.bass_utils import run_bass_kernel_spmd

NCORES = 8
SEQ = 16384
NLOC = 4096
NTILE = NLOC // T
GI = {"ffn1_pre": 0, "ffn1_post": 16, "mix_pre": 32, "mix_post": 48, "mem_pre": 64, "mem_post": 80,
      "ffn2_pre": 96, "ffn2_post": 112, "mem_kv": 128}
GNAMES = ["ffn1_pre_g", "ffn1_post_g", "mix_pre_g", "mix_post_g", "mem_pre_g", "mem_post_g", "ffn2_pre_g", "ffn2_post_g", "mem_kv_g"]

def own_tokens(c):
    return (np.array([4 * i + c for i in range(NLOC // 128)])[:, None] * 128 + np.arange(128)[None, :]).reshape(-1)

def tile_w_n(W, G):
    K, N = W.shape; kc = K // 128; nu = N // (128 * G)
    return np.ascontiguousarray(W.reshape(kc, 128, nu, G, 128).transpose(2, 1, 3, 0, 4).reshape(nu, 128, G * kc * 128))

def tile_w_rhs(W, NB=512):
    K, N = W.shape; kc = K // 128; nb = N // NB
    return np.ascontiguousarray(W.reshape(kc, 128, nb, NB).transpose(2, 1, 0, 3).reshape(nb, 128, kc * NB))

def gcol(g):
    return np.ascontiguousarray(g.reshape(-1, 128).T)

def inproj_cols():
    cuts = np.cumsum([768, 768, 768, 640, 640, 640, 640, 5, 640, 640, 640])
    st = np.concatenate([[0], cuts[:-1]])
    aq, ak, av, bq, bk, bv, bg, bf_, cq, ck, cv = [np.arange(s, e) for s, e in zip(st, cuts)]
    def swp(cols):
        return cols.reshape(-1, 2, 32)[:, ::-1, :].reshape(-1)
    f = []
    for grp in (aq, ak):
        for h in range(6):
            hc = grp[h * 128:(h + 1) * 128]
            f.append(hc); f.append(swp(hc))
    f += [bq, bk, cq, ck, av, bv, cv, bg]
    return np.concatenate(f), bf_

def core_tables(rel_bias, c):
    p = np.arange(128)[:, None]; q = np.arange(128)[None, :]
    cb = np.zeros((5, 128, 8, 128), np.float32); cm = np.zeros((128, 8, 128), np.float32)
    for d_ in range(2):
        for r_ in range(4):
            m = c - r_ + 4 * (1 - d_)
            t = d_ * 4 + r_
            if 0 <= m <= 4:
                idx = np.clip(128 * m + q - p, -256, 256) + 256
                cb[:, :, t, :] = rel_bias[:, idx]
                if m == 0: cm[64:, t, :64] = NEGM
                if m == 4: cm[:64, t, 64:] = NEGM
            else:
                cm[:, t, :] = NEGM
    dm = np.zeros((128, 8, 128), np.float32)
    tri = (p <= q).astype(np.float32)
    chunkm = np.ones((128, 128), np.float32); chunkm[64:, :64] = 0
    dm[:, 4:8, :] = NEGM
    for r_ in range(4):
        if r_ < c: dm[:, r_, :] = 1; dm[:, 4 + r_, :] = 0
        elif r_ == c: dm[:, r_, :] = chunkm; dm[:, 4 + r_, :] = (1 - tri) * NEGM
    oh = np.zeros((128, 4), np.float32); oh[:, c] = 1
    return (np.ascontiguousarray(cb.reshape(5, 128, 1024)), np.ascontiguousarray(cm.reshape(128, 1024)),
            np.ascontiguousarray(dm.reshape(128, 1024)), oh)

def host_consts():
    invf = np.power(np.float32(10000.0), -np.arange(32, dtype=np.float32) / np.float32(32)).astype(np.float32)
    p = np.arange(128)
    rconst = np.stack([invf[p % 32], np.where((p % 64) < 32, -1.0, 1.0), np.zeros(128), np.zeros(128)], axis=1).astype(np.float32)
    tri = (np.arange(128)[:, None] <= np.arange(128)[None, :]).astype(np.float32)
    return rconst, tri, np.eye(128, dtype=np.float32)

def layer_host(inp, l):
    o = {}
    fcols, bfcols = inproj_cols()
    Win = inp["w_in"][l]
    WF = np.concatenate([Win[:, fcols], np.zeros((D, 68 * 128 - fcols.size), np.float32)], axis=1)
    o["winf"] = tile_w_n(WF, 4)
    WT = np.concatenate([Win[:, bfcols], np.zeros((D, 3), np.float32)], axis=1)
    o["wint"] = np.ascontiguousarray(WT.reshape(16, 128, 8).transpose(1, 0, 2).reshape(128, 128))
    for k in ("ffn1", "ffn2"):
        o["wg" + k[-1]] = tile_w_n(inp[k + "_w_gate"][l], 4)
        o["wu" + k[-1]] = tile_w_n(inp[k + "_w_up"][l], 4)
        o["wd" + k[-1]] = tile_w_n(inp[k + "_w_down"][l], 1)
    o["wout"] = tile_w_n(inp["w_out"][l], 4)
    o["wmq"] = tile_w_n(inp["w_mem_q"][l], 4)
    o["wmk"] = tile_w_n(inp["w_mem_kv"][l][:, 0:512], 4)
    o["wmv"] = tile_w_rhs(inp["w_mem_kv"][l][:, 512:1024])
    o["wmo"] = tile_w_n(inp["w_mem_o"][l], 16)
    o["gains"] = np.concatenate([gcol(inp[n][l]) for n in GNAMES], axis=1)
    o["fb"] = np.ascontiguousarray(np.concatenate([np.broadcast_to(inp["fox_forget_b"][l][None, :], (128, 5)), np.zeros((128, 3), np.float32)], axis=1).astype(np.float32))
    lv = np.concatenate([inp["lam_q1"][l], inp["lam_k1"][l], inp["lam_q2"][l], inp["lam_k2"][l]])
    o["lamv"] = np.ascontiguousarray(np.broadcast_to(lv[None, :], (128, 256)).astype(np.float32))
    o["gsub"] = np.ascontiguousarray(inp["diff_subln_g"][l].reshape(128, 1).astype(np.float32))
    return o

class Prog:
    def __init__(self):
        self.nc = bass.Bass("TRN2", target_bir_lowering=False)
        self.dd = {}
    def din(self, name, shape, dt=F32):
        self.dd[name] = self.nc.dram_tensor(name, list(shape), dt, kind="ExternalInput").ap()
        return self.dd[name]
    def dout(self, name, shape, dt=F32):
        self.dd[name] = self.nc.dram_tensor(name, list(shape), dt, kind="ExternalOutput").ap()
        return self.dd[name]

def setup_token_phase(b, c, gains_aps, fb_ap=None, rconst_ap=None):
    setup_common(b, c)
    c.lf = [(b.sbuf("lf%d" % i, [128, 8], F32), Res("lf%d" % i), b.sem("lf%d" % i)) for i in range(2)]; c.lf_n = 0
    c.csem = b.sem("consts")
    c.gts = []
    for i, gap in enumerate(gains_aps):
        gt = b.sbuf("gains%d" % i, [128, 144], F32); g_r = Res("gains%d" % i)
        b.op("sp", lambda e, gt=gt, gap=gap: e.dma_start(out=gt[:], in_=gap), w=[g_r], sem=b.sem("gains%d" % i))
        for c0 in (16, 112):
            b.op("dve", lambda e, gt=gt, c0=c0: e.tensor_scalar(out=gt[:, c0:c0 + 16], in0=gt[:, c0:c0 + 16], scalar1=0.5, scalar2=None, op0=ALU.mult), r=[g_r], w=[g_r])
        c.gts.append((gt, g_r))
    if fb_ap is not None:
        c.fbt = b.sbuf("fbt", [128, 8], F32); c.fb_r = Res("fbt")
        b.op("sp", lambda e: e.dma_start(out=c.fbt[:], in_=fb_ap), w=[c.fb_r], sem=b.sem("fbld"))
        c.tabs = alloc_rope(b, c, rconst_ap)

def setup_mem(b, c):
    mm_ = {"sem": b.sem("memld"),
           "kT": (b.sbuf("memkT", [128, 4, 256], BF16), Res("memkT")),
           "vM": (b.sbuf("memvM", [128, 2, 512], BF16), Res("memvM")),
           "gT": (b.sbuf("gateT", [128, 5, T], BF16), Res("gateT"))}
    return mm_

def t1_stages(b, c, dd, sfx, it, gt, g_r, scr):
    t0 = it * T
    rope_tables(b, c, dd["pos"][:, t0:t0 + T], c.tabs)
    ffn(b, c, dd["wg1" + sfx], dd["wu1" + sfx], dd["wd1" + sfx], (gt, g_r, GI["ffn1_pre"]), (gt, g_r, GI["ffn1_post"]))
    b.op("sp", lambda e: e.dma_start(out=scr["X1"][:, :, t0:t0 + T].rearrange("c p t -> p c t"), in_=c.xt[:]),
         r=c.xt_r, wp=[scr["X1_r"]], sem=c.x1sem)
    prenorm(b, c, (gt, g_r, GI["mix_pre"]))
    inproj(b, c, dd["winf" + sfx], dd["wint" + sfx], c.tabs, c.fbt, c.fb_r, scr, t0, 128 ** -0.5)

def decl_t1_weights(p, sfx):
    p.din("wg1" + sfx, [11, 128, 8192]); p.din("wu1" + sfx, [11, 128, 8192]); p.din("wd1" + sfx, [16, 128, 5632])
    p.din("winf" + sfx, [17, 128, 8192]); p.din("wint" + sfx, [128, 128])

def decl_t2_weights(p):
    p.din("wout", [4, 128, 8192]); p.din("wmq", [1, 128, 8192]); p.din("wmk", [1, 128, 8192]); p.din("wmv", [1, 128, 8192])
    p.din("wmo", [1, 128, 8192]); p.din("wg2", [11, 128, 8192]); p.din("wu2", [11, 128, 8192]); p.din("wd2", [16, 128, 5632])
    p.din("memT", [16, 128, 256])

def decl_t1_outs(p):
    scr = {}
    scr["X1"] = p.dout("X1", [16, 128, NLOC]); scr["KV"] = p.dout("KV", [32, 128, NLOC], BF16)
    scr["QG"] = p.dout("QGo", [21, 128, NLOC], BF16); scr["LF"] = p.dout("LF", [128, NLOC // 128, 5])
    for k in ("X1", "KV", "QG", "LF"):
        scr[k + "_r"] = Res(k)
    return scr

def build_L1():
    p = Prog(); dd = p.dd
    p.din("xT", [16, 128, NLOC]); p.din("pos", [128, NLOC], I32)
    decl_t1_weights(p, "")
    p.din("gains", [128, 144]); p.din("rconst", [128, 4]); p.din("fb", [128, 8])
    scr = decl_t1_outs(p)
    b = Builder(p.nc); c = Ctx()
    setup_token_phase(b, c, [dd["gains"]], dd["fb"], dd["rconst"])
    c.xsem = b.sem("xload"); c.x1sem = b.sem("x1st")
    gt, g_r = c.gts[0]
    for it in range(NTILE):
        t0 = it * T
        b.op("sp", lambda e, t0=t0: e.dma_start(out=c.xt[:], in_=dd["xT"][:, :, t0:t0 + T].rearrange("c p t -> p c t")), w=c.xt_r, sem=c.xsem)
        t1_stages(b, c, dd, "", it, gt, g_r, scr)
    b.finish()
    return p.nc

def build_ATT(lam_init):
    p = Prog(); dd = p.dd
    p.din("KV", [32, 128, SEQ], BF16); p.din("QG", [21, 128, NLOC], BF16); p.din("LF", [128, SEQ // 128, 5])
    p.din("cbias", [5, 128, 1024]); p.din("cmask", [128, 1024]); p.din("idf", [128, 128]); p.din("tri", [128, 128])
    p.din("lamv", [128, 256]); p.din("gsub", [128, 1]); p.din("oh", [128, 4]); p.din("dmask", [128, 1024])
    p.dout("OT", [16, 128, NLOC], BF16); dd["OT_r"] = Res("OT")
    b = Builder(p.nc)
    emit_att(b, dd, SEQ, lam_init)
    b.finish()
    return p.nc

def build_L3(last):
    p = Prog(); dd = p.dd
    p.din("xT", [16, 128, NLOC]); p.din("OT", [16, 128, NLOC], BF16); p.din("QG", [21, 128, NLOC], BF16)
    decl_t2_weights(p)
    p.din("gains", [128, 144])
    if not last:
        p.din("pos", [128, NLOC], I32)
        decl_t1_weights(p, "n")
        p.din("gainsn", [128, 144]); p.din("rconst", [128, 4]); p.din("fb", [128, 8])
        scr = decl_t1_outs(p)
    else:
        p.dout("XO", [16, 128, NLOC])
    b = Builder(p.nc); c = Ctx()
    if not last:
        setup_token_phase(b, c, [dd["gains"], dd["gainsn"]], dd["fb"], dd["rconst"])
    else:
        setup_token_phase(b, c, [dd["gains"]])
    c.xsem = b.sem("xload"); c.x1sem = b.sem("x1st")
    dd["ot_sem"] = b.sem("otld")
    mm_ = setup_mem(b, c)
    gt, g_r = c.gts[0]
    mem_prep(b, c, dd["memT"], dd["wmk"][0], dd["wmv"][0], (gt, g_r, GI["mem_kv"]), mm_)
    XO_r = Res("XO")
    for it in range(NTILE):
        t0 = it * T
        b.op("sp", lambda e, t0=t0: e.dma_start(out=c.xt[:], in_=dd["xT"][:, :, t0:t0 + T].rearrange("c p t -> p c t")), w=c.xt_r, sem=c.xsem)
        t2_stages(b, c, dd, it, gt, g_r, GI, mm_)
        if not last:
            gtn, gn_r = c.gts[1]
            t1_stages(b, c, dd, "n", it, gtn, gn_r, scr)
        else:
            b.op("sp", lambda e, t0=t0: e.dma_start(out=dd["XO"][:, :, t0:t0 + T].rearrange("c p t -> p c t"), in_=c.xt[:]),
                 r=c.xt_r, wp=[XO_r], sem=c.x1sem)
    b.finish()
    return p.nc

def _run(nc, in_maps, tag):
    t0 = _time.time()
    res = run_bass_kernel_spmd(nc, in_maps, core_ids=list(range(NCORES)))
    print("[kernel] launch %s done in %.1fs" % (tag, _time.time() - t0), flush=True)
    return res.results

def gather_seq(outs, key, axis):
    return [np.ascontiguousarray(np.concatenate([outs[bi * 4 + c][key] for c in range(4)], axis=axis)) for bi in range(2)]

def kernel(**inp):
    inp = {k: np.asarray(v) for k, v in inp.items()}
    x = inp["x"]; mem = inp["mem"]; positions = inp["positions"]
    rconst, tri, idf = host_consts()
    tok_idx = []
    for core in range(NCORES):
        tok_idx.append(own_tokens(core % 4))
    L = [layer_host(inp, l) for l in range(2)]
    xT = []; posb = []; memT = []
    for core in range(NCORES):
        bi = core // 4
        xc = x[bi][tok_idx[core]]
        xT.append(np.ascontiguousarray(xc.T.reshape(16, 128, NLOC)))
        posb.append(np.ascontiguousarray(np.broadcast_to(positions[bi][tok_idx[core]][None, :], (128, NLOC)).astype(np.int32)))
        memT.append(np.ascontiguousarray(mem[bi].T.reshape(16, 128, 256)))
    def t1_w(l, sfx):
        return {"wg1" + sfx: L[l]["wg1"], "wu1" + sfx: L[l]["wu1"], "wd1" + sfx: L[l]["wd1"], "winf" + sfx: L[l]["winf"], "wint" + sfx: L[l]["wint"]}
    def t2_w(l):
        return {k: L[l][k] for k in ("wout", "wmq", "wmk", "wmv", "wmo", "wg2", "wu2", "wd2")}
    nc1 = build_L1()
    ins = [dict(xT=xT[i], pos=posb[i], gains=L[0]["gains"], rconst=rconst, fb=L[0]["fb"], **t1_w(0, "")) for i in range(NCORES)]
    o1 = _run(nc1, ins, "T1(0)")
    cur = o1
    xout = None
    for l in range(2):
        lam_init = 0.8 - 0.6 * float(np.exp(-0.3 * l))
        kvf = gather_seq(cur, "KV", 2); lff = gather_seq(cur, "LF", 1)
        nca = build_ATT(lam_init)
        insa = []
        for core in range(NCORES):
            bi = core // 4
            cbt, cmt, dmt, oht = core_tables(inp["chunk_rel_bias"][l], core % 4)
            insa.append(dict(KV=kvf[bi], QG=cur[core]["QGo"], LF=lff[bi], cbias=cbt, cmask=cmt, idf=idf, tri=tri,
                             lamv=L[l]["lamv"], gsub=L[l]["gsub"], oh=oht, dmask=dmt))
        ra = _run(nca, insa, "ATT(%d)" % l)
        outs_att = [ra[i]["OT"] for i in range(NCORES)]
        last = (l == 1)
        nc3 = build_L3(last)
        ins3 = []
        for i in range(NCORES):
            dct = dict(xT=cur[i]["X1"], OT=outs_att[i], QG=cur[i]["QGo"], memT=memT[i], gains=L[l]["gains"], **t2_w(l))
            if not last:
                dct.update(pos=posb[i], gainsn=L[l + 1]["gains"], rconst=rconst, fb=L[l + 1]["fb"], **t1_w(l + 1, "n"))
            ins3.append(dct)
        o3 = _run(nc3, ins3, "T2(%d)" % l)
        if last:
            xout = o3
        else:
            cur = o3
    out = np.empty((2, SEQ, D), np.float32)
    for core in range(NCORES):
        bi = core // 4
        out[bi][tok_idx[core]] = xout[core]["XO"].reshape(D, NLOC).T
    return out
```
